# Optimizing a Trainium2 kernel written in Bass

```python
import jax, jax.numpy as jnp
from jax import lax
import numpy as np

D_MODEL = 4096
BATCH = 8
SEQ = 2048
DEPTH = 1
DEC_BATCH = 32
DEC_SEQ = 16
PAST_LEN = 1024

CHUNK = 64
D_BRANCH = D_MODEL // 2
GMLP_CHUNK = 128
GMLP_GROUPS = 8
GMLP_GDIM = D_BRANCH // GMLP_GROUPS
N_HEADS = 16
HEAD_DIM = D_BRANCH // N_HEADS
PAST_CHUNKS = 8
BAND = (PAST_CHUNKS + 1) * CHUNK
WINDOW = PAST_CHUNKS * CHUNK
REL_CLIP = 128
PLE_DIM = 256
EPS = 1e-6
NEG_INF = -1e30

kernel_name = 'hybrid_gmlp_bandattn_stream_step'


def _rmsnorm(x, g):
    xf = x.astype(jnp.float32)
    y = xf * lax.rsqrt(jnp.mean(xf * xf, axis=-1, keepdims=True) + EPS)
    return (y * g.astype(jnp.float32)).astype(x.dtype)


def _layernorm(x, g, b):
    xf = x.astype(jnp.float32)
    xc = xf - jnp.mean(xf, axis=-1, keepdims=True)
    y = xc * lax.rsqrt(jnp.mean(xc * xc, axis=-1, keepdims=True) + EPS)
    return (y * g.astype(jnp.float32) + b.astype(jnp.float32)).astype(x.dtype)


def _chunk_causal_mask(n):
    c = jnp.arange(n) // CHUNK
    return c[None, :] <= c[:, None]


def _rel_bias(rel_bias, d):
    idx = jnp.clip(d, -REL_CLIP, REL_CLIP) + REL_CLIP
    return jnp.take(rel_bias.astype(jnp.float32), idx, axis=1)


def _project_in(x, pre_g, w_in):
    h = _rmsnorm(x, pre_g)
    sizes = [D_BRANCH] * 7 + [D_MODEL] * 2
    offs = [int(o) for o in np.cumsum(sizes)[:-1]]
    return jnp.split(h @ w_in, offs, axis=-1)


def _gmlp_uv(u, v, ln_g, ln_b):
    u = jax.nn.gelu(u, approximate=False)
    vn = _layernorm(jax.nn.gelu(v, approximate=False), ln_g, ln_b)
    return u, vn


def _sgu_prompt(vn, w_s, b_s):
    B, S, _ = vn.shape
    n = S // GMLP_CHUNK
    w = jnp.where(_chunk_causal_mask(GMLP_CHUNK)[None], w_s, 0)
    vg = vn.reshape(B, n, GMLP_CHUNK, GMLP_GROUPS, GMLP_GDIM)
    out = jnp.einsum('gij,bnjgc->bnigc', w, vg) + b_s.T[None, None, :, :, None]
    return out.reshape(B, S, D_BRANCH)


def _sgu_sample(vn, w_s, b_s):
    B, T, _ = vn.shape
    w = jnp.where(_chunk_causal_mask(T)[None], w_s[:, :T, :T], 0)
    vg = vn.reshape(B, T, GMLP_GROUPS, GMLP_GDIM)
    out = jnp.einsum('gij,bjgc->bigc', w, vg) + b_s[:, :T].T[None, :, :, None]
    return out.reshape(B, T, D_BRANCH)


def _band_attention_prompt(q, k, v, rel_bias):
    B, S, H, Dh = q.shape
    n_c = S // CHUNK
    qc = (q * (Dh ** -0.5)).reshape(B, n_c, CHUNK, H, Dh)
    pad = ((0, 0), (WINDOW, 0), (0, 0), (0, 0))
    kc = jnp.pad(k, pad).reshape(B, n_c + PAST_CHUNKS, CHUNK, H, Dh)
    vc = jnp.pad(v, pad).reshape(B, n_c + PAST_CHUNKS, CHUNK, H, Dh)
    scores = jnp.concatenate(
        [jnp.einsum('bcqhd,bckhd->bhcqk', qc, kc[:, s:s + n_c]) for s in range(PAST_CHUNKS + 1)],
        axis=-1).astype(jnp.float32)
    q_off = jnp.arange(CHUNK)
    k_off = jnp.arange(BAND) - WINDOW
    bias = _rel_bias(rel_bias, q_off[:, None] - k_off[None, :])
    valid = (jnp.arange(n_c)[:, None] + (jnp.arange(BAND) // CHUNK)[None, :] - PAST_CHUNKS) >= 0
    scores = jnp.where(valid[None, None, :, None, :], scores + bias[None, :, None], NEG_INF)
    probs = jax.nn.softmax(scores, axis=-1).astype(v.dtype)
    out = jnp.einsum('bhcqk,bckhd->bcqhd', probs[..., :CHUNK], vc[:, 0:n_c])
    for s in range(1, PAST_CHUNKS + 1):
        out = out + jnp.einsum('bhcqk,bckhd->bcqhd',
                               probs[..., s * CHUNK:(s + 1) * CHUNK], vc[:, s:s + n_c])
    return out.reshape(B, S, H, Dh)


def _band_attention_sample(q, k_new, v_new, k_cache, v_cache, rel_bias):
    Dh = q.shape[-1]
    T = q.shape[1]
    Lc = k_cache.shape[1]
    keys = jnp.concatenate([k_cache, k_new], axis=1)
    vals = jnp.concatenate([v_cache, v_new], axis=1)
    scores = jnp.einsum('bqhd,bkhd->bhqk', q * (Dh ** -0.5), keys).astype(jnp.float32)
    k_off = jnp.concatenate([jnp.arange(Lc) - Lc, jnp.arange(T)])
    bias = _rel_bias(rel_bias, jnp.arange(T)[:, None] - k_off[None, :])
    probs = jax.nn.softmax(scores + bias[None], axis=-1).astype(vals.dtype)
    return jnp.einsum('bhqk,bkhd->bqhd', probs, vals)


def _merge_and_residual(x, y_a, y_b, g_a, g_b, w_up_a, w_up_b, w_out, post_g):
    m = jax.nn.sigmoid(g_a) * (y_a @ w_up_a) + jax.nn.sigmoid(g_b) * (y_b @ w_up_b)
    return x + _rmsnorm(m @ w_out, post_g)


def _ple(x, p, w_pg, w_pp):
    return x + jax.nn.sigmoid(x @ w_pg) * (p @ w_pp)


def _prompt_layer(x, p, pre_g, post_g, w_in, ln_g, ln_b, w_s, b_s, rel_bias,
                  w_up_a, w_up_b, w_out, w_pg, w_pp):
    B, S, _ = x.shape
    u, v, z_a, q, k, val, z_b, g_a, g_b = _project_in(x, pre_g, w_in)
    u, vn = _gmlp_uv(u, v, ln_g, ln_b)
    y_a = u * _sgu_prompt(vn, w_s, b_s) * jax.nn.silu(z_a)
    qh = q.reshape(B, S, N_HEADS, HEAD_DIM)
    kh = k.reshape(B, S, N_HEADS, HEAD_DIM)
    vh = val.reshape(B, S, N_HEADS, HEAD_DIM)
    y_b = _band_attention_prompt(qh, kh, vh, rel_bias).reshape(B, S, D_BRANCH) * jax.nn.silu(z_b)
    x = _merge_and_residual(x, y_a, y_b, g_a, g_b, w_up_a, w_up_b, w_out, post_g)
    x = _ple(x, p, w_pg, w_pp)
    keep = min(WINDOW, S)
    return x, kh[:, S - keep:], vh[:, S - keep:], vn[:, S - GMLP_CHUNK:]


def _sample_layer(x, p, k_cache, v_cache, pre_g, post_g, w_in, ln_g, ln_b, w_s, b_s, rel_bias,
                  w_up_a, w_up_b, w_out, w_pg, w_pp):
    B, T, _ = x.shape
    u, v, z_a, q, k, val, z_b, g_a, g_b = _project_in(x, pre_g, w_in)
    u, vn = _gmlp_uv(u, v, ln_g, ln_b)
    y_a = u * _sgu_sample(vn, w_s, b_s) * jax.nn.silu(z_a)
    qh = q.reshape(B, T, N_HEADS, HEAD_DIM)
    kh = k.reshape(B, T, N_HEADS, HEAD_DIM)
    vh = val.reshape(B, T, N_HEADS, HEAD_DIM)
    y_b = _band_attention_sample(qh, kh, vh, k_cache, v_cache, rel_bias).reshape(B, T, D_BRANCH)
    y_b = y_b * jax.nn.silu(z_b)
    x = _merge_and_residual(x, y_a, y_b, g_a, g_b, w_up_a, w_up_b, w_out, post_g)
    x = _ple(x, p, w_pg, w_pp)
    return x, kh, vh, vn


def setup_inputs(seed: int = 0) -> dict:
    key = jax.random.key(seed)
    ks = jax.random.split(key, 20)

    def nrm(k, shape, scale):
        return jax.random.normal(k, shape, jnp.float32) * scale

    cache_len = min(WINDOW, PAST_LEN)
    d_in = 7 * D_BRANCH + 2 * D_MODEL
    return {
        'x_prompt': nrm(ks[0], (BATCH, SEQ, D_MODEL), 1.0),
        'x_sample': nrm(ks[1], (DEC_BATCH, DEC_SEQ, D_MODEL), 1.0),
        'cache_attn_k': nrm(ks[2], (DEPTH, DEC_BATCH, cache_len, N_HEADS, HEAD_DIM), 1.0),
        'cache_attn_v': nrm(ks[3], (DEPTH, DEC_BATCH, cache_len, N_HEADS, HEAD_DIM), 1.0),
        'p_prompt': nrm(ks[4], (DEPTH, BATCH, SEQ, PLE_DIM), 1.0),
        'p_sample': nrm(ks[5], (DEPTH, DEC_BATCH, DEC_SEQ, PLE_DIM), 1.0),
        'norm_pre_g': 1.0 + nrm(ks[6], (DEPTH, D_MODEL), 0.01),
        'norm_post_g': 1.0 + nrm(ks[7], (DEPTH, D_MODEL), 0.01),
        'w_in': nrm(ks[8], (DEPTH, D_MODEL, d_in), D_MODEL ** -0.5),
        'gmlp_ln_g': 1.0 + nrm(ks[9], (DEPTH, D_BRANCH), 0.01),
        'gmlp_ln_b': nrm(ks[10], (DEPTH, D_BRANCH), 0.01),
        'gmlp_w_s': nrm(ks[11], (DEPTH, GMLP_GROUPS, GMLP_CHUNK, GMLP_CHUNK), 0.5 * GMLP_CHUNK ** -0.5),
        'gmlp_b_s': 1.0 + nrm(ks[12], (DEPTH, GMLP_GROUPS, GMLP_CHUNK), 0.01),
        'attn_rel_bias': nrm(ks[13], (DEPTH, N_HEADS, 2 * REL_CLIP + 1), 0.1),
        'w_up_a': nrm(ks[14], (DEPTH, D_BRANCH, D_MODEL), D_BRANCH ** -0.5),
        'w_up_b': nrm(ks[15], (DEPTH, D_BRANCH, D_MODEL), D_BRANCH ** -0.5),
        'w_out': nrm(ks[16], (DEPTH, D_MODEL, D_MODEL), D_MODEL ** -0.5),
        'w_ple_gate': nrm(ks[17], (DEPTH, D_MODEL, D_MODEL), D_MODEL ** -0.5),
        'w_ple_proj': nrm(ks[18], (DEPTH, PLE_DIM, D_MODEL), PLE_DIM ** -0.5),
    }


def reference(x_prompt, x_sample, cache_attn_k, cache_attn_v, p_prompt, p_sample,
              norm_pre_g, norm_post_g, w_in, gmlp_ln_g, gmlp_ln_b, gmlp_w_s, gmlp_b_s,
              attn_rel_bias, w_up_a, w_up_b, w_out, w_ple_gate, w_ple_proj):
    xp = x_prompt
    xs = x_sample
    kp_l, vp_l, ks_l, vs_l, gp_l, gs_l = [], [], [], [], [], []
    for i in range(DEPTH):
        xp, kp, vp, gp = _prompt_layer(
            xp, p_prompt[i], norm_pre_g[i], norm_post_g[i], w_in[i], gmlp_ln_g[i], gmlp_ln_b[i],
            gmlp_w_s[i], gmlp_b_s[i], attn_rel_bias[i], w_up_a[i], w_up_b[i], w_out[i],
            w_ple_gate[i], w_ple_proj[i])
        xs, kn, vn, gn = _sample_layer(
            xs, p_sample[i], cache_attn_k[i], cache_attn_v[i], norm_pre_g[i], norm_post_g[i],
            w_in[i], gmlp_ln_g[i], gmlp_ln_b[i], gmlp_w_s[i], gmlp_b_s[i], attn_rel_bias[i],
            w_up_a[i], w_up_b[i], w_out[i], w_ple_gate[i], w_ple_proj[i])
        kp_l.append(kp)
        vp_l.append(vp)
        ks_l.append(kn)
        vs_l.append(vn)
        gp_l.append(gp)
        gs_l.append(gn)
    new_k_prompt = jnp.stack(kp_l)
    new_v_prompt = jnp.stack(vp_l)
    new_k_sample = jnp.stack(ks_l)
    new_v_sample = jnp.stack(vs_l)
    gmlp_v_prompt = jnp.stack(gp_l)
    gmlp_v_sample = jnp.stack(gs_l)
    return (xp, xs, new_k_prompt, new_v_prompt, new_k_sample, new_v_sample, gmlp_v_prompt, gmlp_v_sample)
```

```python
import numpy as np
from contextlib import ExitStack
import concourse.bass as bass
import concourse.mybir as mybir
from concourse.bass_utils import run_bass_kernel_spmd

F32 = mybir.dt.float32
BF16 = mybir.dt.bfloat16
AF = mybir.ActivationFunctionType
ALU = mybir.AluOpType

SAME_ENG_SYNC = True
D = 4096
DB = 2048
NH = 16
DIN = 22528
EPS = 1e-6
O_U, O_V, O_ZA, O_Q, O_K, O_VAL, O_ZB, O_GA, O_GB = 0, 2048, 4096, 6144, 8192, 10240, 12288, 14336, 18432
NSLOT = 4
CAST_LAG = 4
DBG = {"tiles": None, "stop": None}


class Buf:
    __slots__ = ("name", "last_w", "readers", "excl")

    def __init__(self, name, excl=False):
        self.name = name
        self.last_w = None
        self.readers = []
        self.excl = excl


class Eng:
    def __init__(self, name, sem):
        self.name = name
        self.sem = sem
        self.count = 0
        self.ops = []
        self.waited = {}


class Prog:
    def __init__(self, nc, stack):
        self.nc = nc
        self.stack = stack
        self.engs = {}
        for n in ("pe", "act", "dve", "pool", "sp"):
            sem = stack.enter_context(nc.semaphore("sem_" + n))
            self.engs[n] = Eng(n, sem)

    def new_sem(self, name):
        return [self.stack.enter_context(self.nc.semaphore(name)), 0]

    def _collect(self, e, reads, writes):
        waits = {}

        def need(tok):
            sem, val, ename = tok
            if ename == e.name and (e.name == "pe" or not SAME_ENG_SYNC):
                return
            key = id(sem)
            if e.waited.get(key, 0) >= val:
                return
            if key not in waits or waits[key][1] < val:
                waits[key] = (sem, val)

        for b in reads:
            if b.last_w is not None:
                need(b.last_w)
            if b.excl:
                for r in b.readers:
                    need(r)
        for b in writes:
            if b.last_w is not None:
                need(b.last_w)
            for r in b.readers:
                need(r)
        for key, (sem, val) in waits.items():
            e.waited[key] = val
        return list(waits.values())

    def _commit(self, tok, reads, writes):
        for b in reads:
            b.readers.append(tok)
            if len(b.readers) > 12:
                best = {}
                for t in b.readers:
                    k = id(t[0])
                    if k not in best or best[k][1] < t[1]:
                        best[k] = t
                b.readers = list(best.values())
        for b in writes:
            b.last_w = tok
            b.readers = []

    def op(self, eng, fn, reads=(), writes=()):
        e = self.engs[eng]
        waits = self._collect(e, reads, writes)
        e.count += 1
        tok = (e.sem, e.count, e.name)
        e.ops.append((waits, fn, e.sem, 1))
        self._commit(tok, reads, writes)
        return tok

    def dma(self, queue, fn, semc, reads=(), writes=()):
        e = self.engs[queue]
        waits = self._collect(e, reads, writes)
        semc[1] += 16
        tok = (semc[0], semc[1], "dma")
        e.ops.append((waits, fn, semc[0], 16))
        self._commit(tok, reads, writes)
        return tok

    def wait_tok(self, eng, tok):
        e = self.engs[eng]
        sem, val, _ = tok
        if e.waited.get(id(sem), 0) >= val:
            return
        e.waited[id(sem)] = val
        e.ops.append(([(sem, val)], None, None, 0))

    def emit(self):
        nc = self.nc
        engs = self.engs

        def replay(e, hw):
            for waits, fn, sem, inc in e.ops:
                for (s, v) in waits:
                    hw.wait_ge(s, v)
                if fn is not None:
                    ins = fn(hw)
                    ins.then_inc(sem, inc)

        with nc.Block() as block:
            @block.tensor
            def _(hw):
                replay(engs["pe"], hw)

            @block.scalar
            def _(hw):
                replay(engs["act"], hw)

            @block.vector
            def _(hw):
                replay(engs["dve"], hw)

            @block.gpsimd
            def _(hw):
                replay(engs["pool"], hw)

            @block.sync
            def _(hw):
                replay(engs["sp"], hw)


class V:
    __slots__ = ("ap", "bufs")

    def __init__(self, ap, bufs):
        self.ap = ap
        self.bufs = bufs

    def __getitem__(self, key):
        return V(self.ap[key], self.bufs)


class Arena:
    CH = 512

    def __init__(self, nc, st, nbytes):
        self.t = st.enter_context(nc.sbuf_tensor("arena", [128, nbytes // 2], BF16))
        self.nbytes = nbytes
        self.bufs = [Buf("a%d" % i) for i in range((nbytes + self.CH - 1) // self.CH)]

    def view(self, off, shape, dt):
        esz = 4 if dt == F32 else 2
        nfree = 1
        for s in shape[1:]:
            nfree *= s
        nb = nfree * esz
        assert off % 4 == 0 and off + nb <= self.nbytes, (off, nb, self.nbytes)
        ap = self.t[0:shape[0], off // 2:(off + nb) // 2]
        if dt == F32:
            ap = ap.bitcast(F32)
        if len(shape) == 3:
            ap = ap.rearrange("p (a b) -> p a b", a=shape[1])
        elif len(shape) == 4:
            ap = ap.rearrange("p (a b c) -> p a b c", a=shape[1], b=shape[2])
        return V(ap, self.bufs[off // self.CH:(off + nb - 1) // self.CH + 1])


def build_program():
    nc = bass.Bass("TRN2", target_bir_lowering=False)
    dt_in = lambda n, s: nc.dram_tensor(n, s, F32, kind="ExternalInput").ap()
    dt_out = lambda n, s: nc.dram_tensor(n, s, F32, kind="ExternalOutput").ap()
    xp = dt_in("xp", [2048, D])
    xs = dt_in("xs", [64, D])
    ck = dt_in("ck", [4, 512, DB])
    cv = dt_in("cv", [4, 512, DB])
    ppr = dt_in("ppr", [2048, 256])
    psm = dt_in("psm", [64, 256])
    gpre = dt_in("gpre", [1, D])
    gpost = dt_in("gpost", [1, D])
    w_in = dt_in("w_in", [D, DIN])
    lng = dt_in("lng", [1, DB])
    lnb = dt_in("lnb", [1, DB])
    ws = dt_in("ws", [8, 128, 128])
    bs = dt_in("bs", [1, 1024])
    rb = dt_in("rb", [16, 257])
    wua = dt_in("wua", [DB, D])
    wub = dt_in("wub", [DB, D])
    wo = dt_in("wo", [D, D])
    wpg = dt_in("wpg", [D, D])
    wpp = dt_in("wpp", [256, D])
    ident = dt_in("ident", [128, 128])
    yp = dt_out("yp", [2048, D])
    ys = dt_out("ys", [64, D])
    nkp = dt_out("nkp", [512, DB])
    nvp = dt_out("nvp", [512, DB])
    nks = dt_out("nks", [64, DB])
    nvs = dt_out("nvs", [64, DB])
    gvp = dt_out("gvp", [128, DB])
    gvs = dt_out("gvs", [64, DB])

    pieces = []

    def S(w, c0):
        pieces.append(("S", w, 0, c0, 4096))

    def S2(w, c0):
        pieces.append(("S2", w, 0, c0, 2048))

    def M(w, r0, c0):
        pieces.append(("M", w, r0, c0, 4096))

    for nbk in range(4):
        for kg in range(4):
            M(w_in, kg * 1024, O_V + nbk * 512)
    for fb in range(16):
        S(w_in, O_U + fb * 128)
        S(w_in, O_ZA + fb * 128)
    for sec in (O_K, O_VAL):
        for nbk in range(4):
            for kg in range(4):
                M(w_in, kg * 1024, sec + nbk * 512)
    for h in range(16):
        S(w_in, O_Q + h * 128)
        S(w_in, O_ZB + h * 128)
    for fb in range(32):
        S(w_in, O_GA + fb * 128)
        S(w_in, O_GB + fb * 128)
        S2(wua, fb * 128)
        S2(wub, fb * 128)
    for fblk in range(8):
        for kg in range(4):
            M(wo, kg * 1024, fblk * 512)
    for fblk in range(8):
        pieces.append(("PP", wpp, 0, fblk * 512, 1024))
        for kg in range(4):
            M(wpg, kg * 1024, fblk * 512)
    NP = len(pieces)

    NPH = (NP + 1) // 2
    wscr_a = nc.dram_tensor("wscr_a", [NPH, 128, 4096], BF16, kind="Internal").ap()
    wscr_b = nc.dram_tensor("wscr_b", [NP - NPH, 128, 4096], BF16, kind="Internal").ap()

    def wpiece(i):
        return wscr_a[i] if i < NPH else wscr_b[i - NPH]
    kscr = nc.dram_tensor("kscr", [128, NH * 512], BF16, kind="Internal").ap()
    vscr = nc.dram_tensor("vscr", [128, 4 * DB], BF16, kind="Internal").ap()

    with ExitStack() as st:
        P = Prog(nc, st)
        KB = 1024
        C0 = 0
        WB0 = 18 * KB
        KV0 = 50 * KB
        R10 = 86 * KB
        R20 = 150 * KB
        T10 = 182 * KB
        T20 = 198 * KB
        TOT = 206 * KB
        A = Arena(nc, st, TOT)
        psum = [st.enter_context(nc.psum_tensor("ps%d" % i, [128, 512], F32)) for i in range(8)]
        psb = [Buf("psb%d" % i, excl=True) for i in range(8)]
        ps_rr = [0]
        ps_live = [False] * 8

        def ps_alloc():
            i = ps_rr[0] % 8
            ps_rr[0] += 1
            assert not ps_live[i], "psum bank %d still live" % i
            ps_live[i] = True
            return i

        def ps_free(i):
            ps_live[i] = False

        def psv(i):
            return V(psum[i][:, :], [psb[i]])

        def psv16(i):
            return V(psum[i][:, :].bitcast(BF16), [psb[i]])

        identb = A.view(C0 + 0, [128, 128], BF16)
        onesb = A.view(C0 + 256, [128, 128], BF16)
        wsT = A.view(C0 + 512, [128, 8, 128], BF16)
        bs2 = A.view(C0 + 2560, [2, 1024], BF16)
        biasT = A.view(C0 + 4608, [128, 16, 256], BF16)
        junk = A.view(C0 + 12800, [128, 512], BF16)
        stat = A.view(C0 + 13824, [128, 64], F32)
        bnst = A.view(C0 + 14080, [128, 4, 6], F32)
        ssq = A.view(C0 + 14208, [128, 4, 8], F32)
        identf = A.view(C0 + 14336, [128, 128], F32)
        cbcol = A.view(C0 + 14848, [128, 16], F32)
        wslots = [A.view(WB0 + i * 8 * KB, [128, 4096], BF16) for i in range(NSLOT)]
        wsems = [P.new_sem("wslot%d" % i) for i in range(NSLOT)]

        s_cast = P.new_sem("cast")
        s_ld = [P.new_sem("ld%d" % i) for i in range(5)]
        s_out = P.new_sem("out")
        s_gv = P.new_sem("gvout")
        s_stg = [P.new_sem("stg%d" % i) for i in range(8)]
        s_prev = [P.new_sem("prev%d" % i) for i in range(2)]
        s_spill = [P.new_sem("spillk"), P.new_sem("spillv")]
        s_gpre = P.new_sem("gpre")
        s_lng = P.new_sem("lng")
        s_lnb = P.new_sem("lnb")
        s_gpost = P.new_sem("gpost")
        s_set = [P.new_sem("set%d" % i) for i in range(4)]
        s_sh = P.new_sem("shift")
        s_tz = P.new_sem("toepl")
        out_toks = []
        castbuf = [Buf("cast%d" % i) for i in range(NP)]
        kscr_b = Buf("kscr")
        vscr_b = Buf("vscr")

        def piece_src(i):
            kind, w, r0, c0, n = pieces[i]
            if kind in ("S", "S2"):
                return w[:, c0:c0 + 128].rearrange("(kc p) c -> p kc c", p=128), 128
            if kind == "M":
                return w[r0:r0 + 1024, c0:c0 + 512].rearrange("(kc p) c -> p kc c", p=128), 512
            return w[:, c0:c0 + 512].rearrange("(kc p) c -> p kc c", p=128), 512
        s_wb = [P.new_sem("wb%d" % i) for i in range(NSLOT)]

        WS = {"issued": 0, "next": 0, "total": 5 * NP}

        def w_issue(j):
            pi = j % NP
            sl = j % NSLOT
            n = pieces[pi][4]
            slot = wslots[sl]
            if j < NP:
                src, cw = piece_src(pi)
                P.dma("pool", lambda e, o=slot.ap[:, 0:n].rearrange("p (kc c) -> p kc c", c=cw), s=src: e.dma_start(out=o, in_=s),
                      wsems[sl], writes=slot.bufs)
            else:
                P.dma("sp", lambda e, o=slot.ap[:, 0:n], s=wpiece(pi)[:, 0:n]: e.dma_start(out=o, in_=s),
                      wsems[sl], reads=[castbuf[pi]], writes=slot.bufs)
            if 2 <= j < NP + 2:
                pw = j - 2
                sw_ = pw % NSLOT
                nw = pieces[pw][4]
                P.dma("pool", lambda e, o=wpiece(pw)[:, 0:nw], s=wslots[sw_].ap[:, 0:nw]: e.dma_start(out=o, in_=s),
                      s_wb[sw_], reads=wslots[sw_].bufs, writes=[castbuf[pw]])

        def w_next(kind):
            i = WS["next"]
            while WS["issued"] <= i + NSLOT - 1 and WS["issued"] < WS["total"]:
                w_issue(WS["issued"])
                WS["issued"] += 1
            WS["next"] += 1
            assert pieces[i % NP][0] == kind, (i, pieces[i % NP][0], kind)
            return wslots[i % NSLOT]

        P.dma("sp", lambda e: e.dma_start(out=identf.ap, in_=ident), s_set[0], writes=identf.bufs)
        P.op("dve", lambda e: e.tensor_copy(out=identb.ap, in_=identf.ap), reads=identf.bufs, writes=identb.bufs)
        P.op("dve", lambda e: e.memset(onesb.ap, 1.0), writes=onesb.bufs)
        wsf = A.view(R20, [128, 8, 128], F32)
        wsb_ = A.view(R20 + 4096, [128, 8, 128], BF16)
        P.dma("sp", lambda e: e.dma_start(out=wsf.ap, in_=ws.rearrange("g i j -> i g j")), s_set[1], writes=wsf.bufs)
        P.op("dve", lambda e: e.tensor_copy(out=wsb_.ap, in_=wsf.ap), reads=wsf.bufs, writes=wsb_.bufs)
        pi0 = ps_alloc()
        p16 = psv16(pi0)

        def tr_ws(e):
            for g in range(8):
                ins = e.transpose(out=p16.ap[:, g * 128:(g + 1) * 128], in_=wsb_.ap[:, g, :], identity=identb.ap)
            return ins
        P.op("pe", tr_ws, reads=wsb_.bufs + identb.bufs, writes=p16.bufs)
        P.op("dve", lambda e: e.tensor_copy(out=wsT.ap, in_=p16.ap.rearrange("p (g i) -> p g i", g=8)),
             reads=p16.bufs, writes=wsT.bufs)
        ps_free(pi0)
        P.op("dve", lambda e: e.memset(wsT.ap[64:128, :, 0:64], 0.0), writes=wsT.bufs)
        bsf = A.view(R20 + 8192, [2, 1024], F32)
        bshi = A.view(R20 + 12288, [2, 1024], BF16)
        bshf = A.view(R20 + 16384, [2, 1024], F32)
        P.dma("sp", lambda e: e.dma_start(out=bsf.ap, in_=bs.partition_broadcast(2)), s_set[2], writes=bsf.bufs)
        P.op("dve", lambda e: e.tensor_copy(out=bshi.ap, in_=bsf.ap), reads=bsf.bufs, writes=bshi.bufs)
        P.op("dve", lambda e: e.tensor_copy(out=bshf.ap, in_=bshi.ap), reads=bshi.bufs, writes=bshf.bufs)
        P.op("dve", lambda e: e.tensor_tensor(out=bs2.ap, in0=bsf.ap, in1=bshf.ap, op=ALU.subtract),
             reads=bsf.bufs + bshf.bufs, writes=bs2.bufs)
        P.op("dve", lambda e: e.tensor_copy(out=bs2.ap[0:1, :], in_=bshi.ap[0:1, :]), reads=bshi.bufs, writes=bs2.bufs)
        Tf = A.view(R10, [128, 16, 256], F32)
        P.dma("sp", lambda e: e.dma_start(out=cbcol.ap, in_=rb[:, 256:257].rearrange("h o -> o h").partition_broadcast(128),
                                          allow_slow_non_contiguous=True), s_set[3], writes=cbcol.bufs)
        P.op("dve", lambda e: e.tensor_copy(out=Tf.ap, in_=cbcol.ap.unsqueeze(2).to_broadcast([128, 16, 256])),
             reads=cbcol.bufs, writes=Tf.bufs)
        for p in range(128):
            n_p = min(129 + p, 256)
            P.dma("sp", lambda e, p=p, n_p=n_p: e.dma_start(out=Tf.ap[p:p + 1, :, 0:n_p],
                                                          in_=rb[:, 128 - p:128 - p + n_p].unsqueeze(0)),
                  s_tz, reads=[], writes=Tf.bufs if p == 0 else [])
        for b in Tf.bufs:
            b.last_w = (s_tz[0], s_tz[1], "dma")
        P.op("dve", lambda e: e.tensor_tensor(out=biasT.ap, in0=Tf.ap, in1=cbcol.ap.unsqueeze(2).to_broadcast([128, 16, 256]),
                                              op=ALU.subtract), reads=Tf.bufs + cbcol.bufs, writes=biasT.bufs)

        ev_rr = [0]

        def ev_eng():
            ev_rr[0] += 1
            return "act" if ev_rr[0] % 2 == 0 else "dve"

        def copy_op(eng, out, in_, scale=None):
            if eng == "act":
                if scale is None:
                    P.op("act", lambda e: e.copy(out=out.ap, in_=in_.ap), reads=in_.bufs, writes=out.bufs)
                else:
                    P.op("act", lambda e: e.mul(out=out.ap, in_=in_.ap, mul=scale), reads=in_.bufs, writes=out.bufs)
            else:
                if scale is None:
                    P.op("dve", lambda e: e.tensor_copy(out=out.ap, in_=in_.ap), reads=in_.bufs, writes=out.bufs)
                else:
                    P.op("dve", lambda e: e.tensor_scalar_mul(out=out.ap, in0=in_.ap, scalar1=scale),
                         reads=in_.bufs, writes=out.bufs)

        def act_fn(out, in_, func, accum=None):
            if accum is None:
                P.op("act", lambda e: e.activation(out=out.ap, in_=in_.ap, func=func), reads=in_.bufs, writes=out.bufs)
            else:
                P.op("act", lambda e: e.activation(out=out.ap, in_=in_.ap, func=func, accum_out=accum.ap),
                     reads=in_.bufs, writes=out.bufs + accum.bufs)

        def tt(out, in0, in1, op, eng="dve"):
            P.op(eng, lambda e: e.tensor_tensor(out=out.ap, in0=in0.ap, in1=in1.ap, op=op),
                 reads=in0.bufs + in1.bufs, writes=out.bufs)

        def rstd_from(dst, src, scale, nrows):
            P.op("dve", lambda e: e.tensor_scalar(out=dst.ap, in0=src.ap, scalar1=scale, scalar2=EPS,
                                                  op0=ALU.mult, op1=ALU.add), reads=src.bufs, writes=dst.bufs)
            P.op("act", lambda e: e.activation(out=dst.ap, in_=dst.ap, func=AF.Sqrt), reads=dst.bufs, writes=dst.bufs)
            P.op("dve", lambda e: e.reciprocal(out=dst.ap, in_=dst.ap), reads=dst.bufs, writes=dst.bufs)

        def bcast_load(dstv, src_row, sem):
            P.dma("sp", lambda e: e.dma_start(out=dstv.ap, in_=src_row.partition_broadcast(128)), sem, writes=dstv.bufs)

        def run_tile(kind, ti):
            sample = kind == "s"
            W = 64 if sample else 512
            tbs = [(0, 64)] if sample else [(i * 128, 128) for i in range(4)]
            ntb = len(tbs)
            xsrc = xs if sample else xp[ti * 512:(ti + 1) * 512, :]
            psrc = psm if sample else ppr[ti * 512:(ti + 1) * 512, :]
            ysrc = ys if sample else yp[ti * 512:(ti + 1) * 512, :]
            out_kv = sample or ti == 3
            has_prev = (not sample) and ti > 0
            do_spill = (not sample) and ti < 3
            nk_out = nks if sample else nkp
            nv_out = nvs if sample else nvp

            hT = A.view(R10, [128, 32, W], BF16)
            if sample:
                y_a = A.view(R10 + 4 * KB, [128, 16, W], BF16)
                y_b = A.view(R10 + 6 * KB, [128, 16, W], BF16)
                vn = A.view(R10 + 8 * KB, [128, 1, DB], BF16)
                vnb = A.view(R10 + 12 * KB, [16, 4, DB], BF16)
                Vnb = A.view(R10 + 28 * KB, [16, 4, DB], BF16)
                PTs = [A.view(R10 + 44 * KB + i * 3 * KB, [128, 5, 16, 16], BF16) for i in range(2)]
            else:
                y_a = A.view(R10 + 32 * KB, [128, 16, W], BF16)
                vn = A.view(R10 + 48 * KB, [128, 4, DB], BF16)
                y_b = A.view(R10 + 48 * KB, [128, 16, W], BF16)
            r_ = A.view(R10, [128, ntb, D], F32)
            mT = A.view(R20, [128, 32, W], BF16)
            x1T = mT
            pT = A.view(C0 + 16 * KB, [128, 2, W], BF16)
            kT = A.view(KV0, [128, 16, W], BF16)
            Vsb = A.view(KV0 + 16 * KB, [128, ntb, DB], BF16)

            gpre_b = A.view(T10, [128, D], F32)
            bcast_load(gpre_b, gpre, s_gpre)
            xsb = A.view(T20, [128, D], BF16)
            pf = A.view(KV0 + 32 * KB, [128, 256], F32)
            pb = A.view(KV0 + 33 * KB, [128, 256], BF16)
            for bi, (r0, nr) in enumerate(tbs):
                xblk = A.view(R20 + (bi % 2) * 16 * KB, [128, D], F32)
                xb = xblk[0:nr]
                P.dma("sp", lambda e, o=xb.ap, s=xsrc[r0:r0 + nr, :]: e.dma_start(out=o, in_=s), s_ld[bi % 2], writes=xb.bufs)
                if DBG["stop"] == "P" and DBG.get("sub") == 0:
                    continue
                sscol = stat[0:nr, 0:1]
                rcol = stat[0:nr, 1:2]
                act_fn(xsb[0:nr], xb, AF.Square, accum=sscol)
                rstd_from(rcol, sscol, 1.0 / D, nr)
                P.op("dve", lambda e, o=xsb[0:nr].ap, i0=xb.ap, sc=rcol.ap, i1=gpre_b[0:nr].ap:
                     e.scalar_tensor_tensor(out=o, in0=i0, scalar=sc, in1=i1, op0=ALU.mult, op1=ALU.mult),
                     reads=xb.bufs + rcol.bufs + gpre_b.bufs, writes=xsb.bufs)
                if DBG["stop"] == "P" and DBG.get("sub") == 1:
                    continue
                for q4 in range(4):
                    pi = ps_alloc()
                    p16 = psv16(pi)

                    def trx(e, q4=q4, p16=p16, nr=nr):
                        for j in range(8):
                            kc = q4 * 8 + j
                            ins = e.transpose(out=p16.ap[:, j * nr:(j + 1) * nr], in_=xsb.ap[0:nr, kc * 128:(kc + 1) * 128],
                                              identity=identb.ap[0:nr, 0:nr])
                        return ins
                    P.op("pe", trx, reads=xsb.bufs + identb.bufs, writes=p16.bufs)
                    copy_op(ev_eng(), hT[:, q4 * 8:(q4 + 1) * 8, r0:r0 + nr],
                            V(p16.ap[:, 0:8 * nr].rearrange("p (k c) -> p k c", k=8), p16.bufs))
                    ps_free(pi)
                if DBG["stop"] == "P" and DBG.get("sub") == 2:
                    continue
                P.dma("sp", lambda e, o=pf[0:nr].ap, s=psrc[r0:r0 + nr, :]: e.dma_start(out=o, in_=s), s_ld[2], writes=pf.bufs)
                copy_op("dve", pb[0:nr], pf[0:nr])
                pi = ps_alloc()
                p16 = psv16(pi)

                def trp(e, p16=p16, nr=nr):
                    for j in range(2):
                        ins = e.transpose(out=p16.ap[:, j * nr:(j + 1) * nr], in_=pb.ap[0:nr, j * 128:(j + 1) * 128],
                                          identity=identb.ap[0:nr, 0:nr])
                    return ins
                P.op("pe", trp, reads=pb.bufs + identb.bufs, writes=p16.bufs)
                copy_op("act", pT[:, :, r0:r0 + nr], V(p16.ap[:, 0:2 * nr].rearrange("p (k c) -> p k c", k=2), p16.bufs))
                ps_free(pi)

            if DBG["stop"] == "P":
                return
            def proj_tm(actT, nkg, evac):
                banks = [ps_alloc() for _ in tbs]
                for kg in range(nkg):
                    slot = w_next("M")
                    wv = slot.ap.rearrange("p (kc c) -> p kc c", c=512)

                    def mm(e, kg=kg, wv=wv):
                        for kc in range(8):
                            for bi, (r0, nr) in enumerate(tbs):
                                ins = e.matmul(psum[banks[bi]][0:nr, :], lhsT=actT.ap[:, kg * 8 + kc, r0:r0 + nr], rhs=wv[:, kc, :],
                                               start=(kg == 0 and kc == 0), stop=(kg == nkg - 1 and kc == 7))
                        return ins
                    P.op("pe", mm, reads=actT.bufs + slot.bufs, writes=[psb[b] for b in banks])
                for bi in range(ntb):
                    evac(bi, psv(banks[bi]))
                    ps_free(banks[bi])

            def proj_fm(actT, nkc, kind_):
                slot = w_next(kind_)
                wv = slot.ap[:, 0:nkc * 128].rearrange("p (kc c) -> p kc c", c=128)
                bank = ps_alloc()

                def mm(e):
                    for kc in range(nkc):
                        ins = e.matmul(psum[bank][:, 0:W], lhsT=wv[:, kc, :], rhs=actT.ap[:, kc, :], start=(kc == 0), stop=(kc == nkc - 1))
                    return ins
                P.op("pe", mm, reads=actT.bufs + slot.bufs, writes=[psb[bank]])
                return bank

            gv = A.view(R20, [128, ntb, DB], F32)
            lng_b = A.view(T10, [128, DB], F32)
            lnb_b = A.view(T10 + 8 * KB, [128, DB], F32)
            bcast_load(lng_b, lng, s_lng)
            bcast_load(lnb_b, lnb, s_lnb)
            for nbk in range(4):
                def ev_v(bi, pv, nbk=nbk):
                    nr = tbs[bi][1]
                    act_fn(gv[0:nr, bi, nbk * 512:(nbk + 1) * 512], pv[0:nr], AF.Gelu)
                proj_tm(hT, 4, ev_v)
            for bi, (r0, nr) in enumerate(tbs):
                g_ = gv[0:nr, bi, :]
                for a in range(4):
                    P.op("dve", lambda e, a=a, g_=g_, nr=nr: e.bn_stats(out=bnst.ap[0:nr, a, :], in_=g_.ap[:, a * 512:(a + 1) * 512]),
                         reads=g_.bufs, writes=bnst.bufs)
                mv = stat[0:nr, 2:4]
                P.op("dve", lambda e, mv=mv, nr=nr: e.bn_aggr(out=mv.ap, in_=bnst.ap[0:nr].rearrange("p a b -> p (a b)")),
                     reads=bnst.bufs, writes=mv.bufs)
                rl = stat[0:nr, 4:5]
                rstd_from(rl, stat[0:nr, 3:4], 1.0, nr)
                P.op("dve", lambda e, g_=g_, nr=nr, rl=rl: e.tensor_scalar(out=g_.ap, in0=g_.ap, scalar1=stat.ap[0:nr, 2:3], scalar2=rl.ap,
                                                                    op0=ALU.subtract, op1=ALU.mult),
                     reads=g_.bufs + stat.bufs, writes=g_.bufs)
                tt(g_, g_, lng_b[0:nr], ALU.mult)
                tt(g_, g_, lnb_b[0:nr], ALU.add)
                copy_op("act", vn[0:nr, bi, :], g_)
                if sample:
                    out_toks.append(P.dma("pool", lambda e, s=g_.ap: e.dma_start(out=gvs, in_=s), s_gv, reads=g_.bufs))
                elif ti == 3 and bi == 3:
                    out_toks.append(P.dma("pool", lambda e, s=g_.ap: e.dma_start(out=gvp, in_=s), s_gv, reads=g_.bufs))
            if sample:
                for b in range(4):
                    P.dma("sp", lambda e, b=b: e.dma_start(out=vnb.ap[:, b, :], in_=vn.ap[16 * b:16 * b + 16, 0, :]), s_sh,
                          reads=vn.bufs, writes=vnb.bufs)

            if DBG["stop"] == "A1":
                return
            for fb in range(16):
                g = fb // 2
                bu = proj_fm(hT, 32, "S")
                gu = A.view(T20 + (fb % 2) * 3 * KB, [128, W], BF16)
                sz = A.view(T20 + (fb % 2) * 3 * KB + KB, [128, W], BF16)
                t1 = A.view(T20 + (fb % 2) * 3 * KB + 2 * KB, [128, W], BF16)
                act_fn(gu, psv(bu)[:, 0:W], AF.Gelu)
                ps_free(bu)
                bz = proj_fm(hT, 32, "S")
                act_fn(sz, psv(bz)[:, 0:W], AF.Silu)
                ps_free(bz)
                bs_ = ps_alloc()

                def sgu(e, fb=fb, g=g, bs_=bs_):
                    if sample:
                        for b in range(4):
                            e.matmul(psum[bs_][:, 16 * b:16 * b + 16], lhsT=vnb.ap[0:16, b, fb * 128:(fb + 1) * 128],
                                     rhs=wsT.ap[0:16, g, 0:16], start=True, stop=False)
                            ins = e.matmul(psum[bs_][:, 16 * b:16 * b + 16], lhsT=onesb.ap[0:2, :], rhs=bs2.ap[0:2, g * 128:g * 128 + 16],
                                           start=False, stop=True)
                    else:
                        for ch in range(4):
                            e.matmul(psum[bs_][:, ch * 128:(ch + 1) * 128], lhsT=vn.ap[:, ch, fb * 128:(fb + 1) * 128],
                                     rhs=wsT.ap[:, g, :], start=True, stop=False)
                            ins = e.matmul(psum[bs_][:, ch * 128:(ch + 1) * 128], lhsT=onesb.ap[0:2, :], rhs=bs2.ap[0:2, g * 128:(g + 1) * 128],
                                           start=False, stop=True)
                    return ins
                P.op("pe", sgu, reads=(vnb.bufs if sample else vn.bufs) + wsT.bufs + onesb.bufs + bs2.bufs, writes=[psb[bs_]])
                tt(t1, psv(bs_)[:, 0:W], gu, ALU.mult)
                ps_free(bs_)
                tt(y_a[:, fb, :], t1, sz, ALU.mult)

            if DBG["stop"] == "A2":
                return
            ktm = A.view(R20, [128, ntb, 512], BF16)
            stg = [A.view(R20 + 4 * KB + i * 2 * KB, [128, 512], F32) for i in range(8)]
            stg_rr = [0]
            for nbk in range(4):
                def ev_k(bi, pv, nbk=nbk):
                    r0, nr = tbs[bi]
                    copy_op("act", ktm[0:nr, bi, :], pv[0:nr])
                    if out_kv:
                        si = stg_rr[0] % 8
                        sg = stg[si]
                        stg_rr[0] += 1
                        copy_op("dve", sg[0:nr], pv[0:nr])
                        out_toks.append(P.dma("pool", lambda e, s=sg[0:nr].ap, o=nk_out[r0:r0 + nr, nbk * 512:(nbk + 1) * 512]:
                                              e.dma_start(out=o, in_=s), s_stg[si], reads=sg.bufs))
                proj_tm(hT, 4, ev_k)
                for bi, (r0, nr) in enumerate(tbs):
                    pi = ps_alloc()
                    p16 = psv16(pi)

                    def trk(e, bi=bi, nr=nr, p16=p16):
                        for j in range(4):
                            ins = e.transpose(out=p16.ap[:, j * nr:(j + 1) * nr], in_=ktm.ap[0:nr, bi, j * 128:(j + 1) * 128],
                                              identity=identb.ap[0:nr, 0:nr])
                        return ins
                    P.op("pe", trk, reads=ktm.bufs + identb.bufs, writes=p16.bufs)
                    copy_op(ev_eng(), kT[:, nbk * 4:(nbk + 1) * 4, r0:r0 + nr],
                            V(p16.ap[:, 0:4 * nr].rearrange("p (k c) -> p k c", k=4), p16.bufs))
                    ps_free(pi)
            for nbk in range(4):
                def ev_vv(bi, pv, nbk=nbk):
                    r0, nr = tbs[bi]
                    copy_op("act", Vsb[0:nr, bi, nbk * 512:(nbk + 1) * 512], pv[0:nr])
                    if out_kv:
                        si = stg_rr[0] % 8
                        sg = stg[si]
                        stg_rr[0] += 1
                        copy_op("dve", sg[0:nr], pv[0:nr])
                        out_toks.append(P.dma("pool", lambda e, s=sg[0:nr].ap, o=nv_out[r0:r0 + nr, nbk * 512:(nbk + 1) * 512]:
                                              e.dma_start(out=o, in_=s), s_stg[si], reads=sg.bufs))
                proj_tm(hT, 4, ev_vv)
            if sample:
                for b in range(4):
                    P.dma("sp", lambda e, b=b: e.dma_start(out=Vnb.ap[:, b, :], in_=Vsb.ap[16 * b:16 * b + 16, 0, :]), s_sh,
                          reads=Vsb.bufs, writes=Vnb.bufs)

            if DBG["stop"] == "B1":
                return
            scale = 128.0 ** -0.5
            qTs = [A.view(T20 + i * KB, [128, W], BF16) for i in range(2)]
            szbs = [A.view(T20 + 2 * KB + i * KB, [128, W], BF16) for i in range(2)]
            rec = A.view(T20 + 4 * KB, [128, W], F32)
            otmp = A.view(T20 + 6 * KB, [128, W], F32)

            def proj_qz(h):
                bq = proj_fm(hT, 32, "S")
                copy_op("dve", qTs[h % 2], psv(bq)[:, 0:W], scale=scale)
                ps_free(bq)
                bz = proj_fm(hT, 32, "S")
                act_fn(szbs[h % 2], psv(bz)[:, 0:W], AF.Silu)
                ps_free(bz)

            if not sample:
                NJ = [128, 256, 384, 512, 512, 384, 256, 128]
                offs = [0]
                for n in NJ:
                    offs.append(offs[-1] + n)
                PTset = [[A.view(R20 + 20 * KB + s_ * 5 * KB + offs[j] * 2, [128, NJ[j]], BF16) for j in range(8)] for s_ in range(2)]
                allPT = A.view(R20 + 20 * KB, [128, 5120], BF16)
                P.op("dve", lambda e: e.memset(allPT.ap, 0.0), writes=allPT.bufs)
                kprev = [A.view(KV0 + 32 * KB + i * 2 * KB, [128, 512], BF16) for i in range(2)]
                vprev = [A.view(KV0 + 32 * KB + i * 2 * KB + KB, [128, 4, 128], BF16) for i in range(2)]

                def load_prev(h):
                    P.dma("sp", lambda e, o=kprev[h % 2].ap, s=kscr[:, h * 512:(h + 1) * 512]: e.dma_start(out=o, in_=s),
                          s_prev[h % 2], reads=[kscr_b], writes=kprev[h % 2].bufs)
                    P.dma("sp", lambda e, o=vprev[h % 2].ap, s=vscr.rearrange("p (t f) -> p t f", t=4)[:, :, h * 128:(h + 1) * 128]:
                          e.dma_start(out=o, in_=s), s_prev[h % 2], reads=[vscr_b], writes=vprev[h % 2].bufs)

                jlist = list(range(8)) if has_prev else [4, 5, 6, 7]
                if has_prev:
                    load_prev(0)
                proj_qz(0)
                for h in range(NH):
                    qT = qTs[h % 2]
                    PT = PTset[h % 2]
                    if has_prev and h + 1 < NH:
                        load_prev(h + 1)
                    for j in jlist:
                        N = NJ[j]
                        if j < 4:
                            kblk = kprev[h % 2][:, j * 128:(j + 1) * 128]
                            q0 = 0
                            nbias = 128 if j == 3 else 0
                            b0 = 128
                        else:
                            kblk = kT[:, h, (j - 4) * 128:(j - 3) * 128]
                            q0 = (j - 4) * 128
                            nbias = min(N, 256)
                            b0 = 0
                        bank = ps_alloc()

                        def smm(e, bank=bank, kblk=kblk, q0=q0, N=N, nbias=nbias, b0=b0, qT=qT, h=h):
                            ins = None
                            if nbias:
                                e.matmul(psum[bank][:, 0:nbias], lhsT=identb.ap, rhs=biasT.ap[:, h, b0:b0 + nbias], start=True, stop=False)
                                ins = e.matmul(psum[bank][:, 0:nbias], lhsT=kblk.ap, rhs=qT.ap[:, q0:q0 + nbias], start=False, stop=True)
                            if N > nbias:
                                ins = e.matmul(psum[bank][:, nbias:N], lhsT=kblk.ap, rhs=qT.ap[:, q0 + nbias:q0 + N], start=True, stop=True)
                            return ins
                        P.op("pe", smm, reads=kblk.bufs + qT.bufs + identb.bufs + biasT.bufs, writes=[psb[bank]])
                        pv = psv(bank)
                        if j < 4:
                            act_fn(PT[j][:, 0:N - 64], pv[:, 0:N - 64], AF.Exp)
                            act_fn(PT[j][64:128, N - 64:N], pv[64:128, N - 64:N], AF.Exp)
                        else:
                            act_fn(PT[j][:, 64:N], pv[:, 64:N], AF.Exp)
                            act_fn(PT[j][0:64, 0:64], pv[0:64, 0:64], AF.Exp)
                        ps_free(bank)
                    if h + 1 < NH:
                        proj_qz(h + 1)
                    bo = ps_alloc()
                    bd = ps_alloc()
                    order = ([3] + [j for j in jlist if j != 3]) if has_prev else [4, 5, 6, 7]

                    def pvmm(e, bo=bo, bd=bd, order=order, PT=PT, h=h):
                        for idx, j in enumerate(order):
                            N = NJ[j]
                            q0 = 0 if j < 4 else (j - 4) * 128
                            vb = vprev[h % 2].ap[:, j, :] if j < 4 else Vsb.ap[:, j - 4, h * 128:(h + 1) * 128]
                            e.matmul(psum[bo][:, q0:q0 + N], lhsT=vb, rhs=PT[j].ap, start=(idx == 0), stop=(idx == len(order) - 1))
                        for idx, j in enumerate(order):
                            N = NJ[j]
                            q0 = 0 if j < 4 else (j - 4) * 128
                            ins = e.matmul(psum[bd][:, q0:q0 + N], lhsT=onesb.ap, rhs=PT[j].ap, start=(idx == 0), stop=(idx == len(order) - 1))
                        return ins
                    rd = Vsb.bufs + onesb.bufs + vprev[h % 2].bufs
                    for j in order:
                        rd = rd + PT[j].bufs
                    P.op("pe", pvmm, reads=rd, writes=[psb[bo], psb[bd]])
                    P.op("dve", lambda e, bd=bd: e.reciprocal(out=rec.ap, in_=psum[bd][:, 0:W]), reads=[psb[bd]], writes=rec.bufs)
                    tt(otmp, psv(bo)[:, 0:W], rec, ALU.mult)
                    ps_free(bo)
                    ps_free(bd)
                    tt(y_b[:, h, :], otmp, szbs[h % 2], ALU.mult)
            else:
                cst = [A.view(T10 + i * 8 * KB, [128, DB], F32) for i in range(2)]
                kbb = A.view(T20 + 2 * KB, [128, DB], BF16)
                Vc = [A.view(R20 + i * 16 * KB, [128, 4, DB], BF16) for i in range(2)]
                kTc = [A.view(KV0 + 8 * KB + i * 4 * KB, [128, 16, 128], BF16) for i in range(2)]
                cnt = 0
                for h in range(NH):
                    bq = proj_fm(hT, 32, "S")
                    copy_op("dve", qall[:, h, :], psv(bq)[:, 0:W], scale=scale)
                    ps_free(bq)
                    bz = proj_fm(hT, 32, "S")
                    act_fn(szall[:, h, :], psv(bz)[:, 0:W], AF.Silu)
                    ps_free(bz)
                for b in range(4):
                    PTb = PTs[b % 2]
                    vcb = Vc[b % 2]
                    for jb in range(4):
                        c_ = cst[cnt % 2]
                        cnt += 1
                        P.dma("sp", lambda e, o=c_.ap, s=cv[b, jb * 128:(jb + 1) * 128, :]: e.dma_start(out=o, in_=s), s_ld[3 + (cnt - 1) % 2], writes=c_.bufs)
                        copy_op("dve", vcb[:, jb, :], c_)
                    for jb in range(4):
                        c_ = cst[cnt % 2]
                        cnt += 1
                        P.dma("sp", lambda e, o=c_.ap, s=ck[b, jb * 128:(jb + 1) * 128, :]: e.dma_start(out=o, in_=s), s_ld[3 + (cnt - 1) % 2], writes=c_.bufs)
                        copy_op("act", kbb, c_)
                        kt_ = kTc[jb % 2]
                        for q4 in range(2):
                            pi = ps_alloc()
                            p16 = psv16(pi)

                            def trc(e, q4=q4, p16=p16):
                                for j in range(8):
                                    hh = q4 * 8 + j
                                    ins = e.transpose(out=p16.ap[:, j * 128:(j + 1) * 128], in_=kbb.ap[:, hh * 128:(hh + 1) * 128], identity=identb.ap)
                                return ins
                            P.op("pe", trc, reads=kbb.bufs + identb.bufs, writes=p16.bufs)
                            copy_op(ev_eng(), kt_[:, q4 * 8:(q4 + 1) * 8, :], V(p16.ap.rearrange("p (k c) -> p k c", k=8), p16.bufs))
                            ps_free(pi)
                        bank = ps_alloc()

                        def scm(e, bank=bank, kt_=kt_, b=b, jb=jb):
                            for hh in range(NH):
                                o = psum[bank][:, hh * 16:(hh + 1) * 16]
                                qa = qall.ap[:, hh, 16 * b:16 * b + 16]
                                if jb == 3:
                                    e.matmul(o, lhsT=identb.ap, rhs=biasT.ap[:, hh, 128:144], start=True, stop=False)
                                    ins = e.matmul(o, lhsT=kt_.ap[:, hh, :], rhs=qa, start=False, stop=True)
                                else:
                                    ins = e.matmul(o, lhsT=kt_.ap[:, hh, :], rhs=qa, start=True, stop=True)
                            return ins
                        P.op("pe", scm, reads=kt_.bufs + qall.bufs + identb.bufs + biasT.bufs, writes=[psb[bank]])
                        act_fn(V(PTb.ap[:, jb, :, :], PTb.bufs), V(psum[bank][:, 0:256].rearrange("p (h q) -> p h q", h=16), [psb[bank]]), AF.Exp)
                        ps_free(bank)
                    bank = ps_alloc()

                    def scn(e, bank=bank, b=b):
                        for hh in range(NH):
                            o = psum[bank][0:16, hh * 16:(hh + 1) * 16]
                            e.matmul(o, lhsT=identb.ap[0:16, 0:16], rhs=biasT.ap[0:16, hh, 0:16], start=True, stop=False)
                            ins = e.matmul(o, lhsT=kT.ap[:, hh, 16 * b:16 * b + 16], rhs=qall.ap[:, hh, 16 * b:16 * b + 16], start=False, stop=True)
                        return ins
                    P.op("pe", scn, reads=kT.bufs + qall.bufs + identb.bufs + biasT.bufs, writes=[psb[bank]])
                    act_fn(V(PTb.ap[0:16, 4, :, :], PTb.bufs), V(psum[bank][0:16, 0:256].rearrange("p (h q) -> p h q", h=16), [psb[bank]]), AF.Exp)
                    ps_free(bank)
                    bo = ps_alloc()
                    bd = ps_alloc()

                    def pvs(e, bo=bo, bd=bd, b=b, PTb=PTb, vcb=vcb):
                        for hh in range(NH):
                            o = psum[bo][:, hh * 16:(hh + 1) * 16]
                            for jb in range(4):
                                e.matmul(o, lhsT=vcb.ap[:, jb, hh * 128:(hh + 1) * 128], rhs=PTb.ap[:, jb, hh, :], start=(jb == 0), stop=False)
                            e.matmul(o, lhsT=Vnb.ap[0:16, b, hh * 128:(hh + 1) * 128], rhs=PTb.ap[0:16, 4, hh, :], start=False, stop=True)
                        for hh in range(NH):
                            o = psum[bd][:, hh * 16:(hh + 1) * 16]
                            for jb in range(4):
                                e.matmul(o, lhsT=onesb.ap, rhs=PTb.ap[:, jb, hh, :], start=(jb == 0), stop=False)
                            ins = e.matmul(o, lhsT=onesb.ap[0:16, :], rhs=PTb.ap[0:16, 4, hh, :], start=False, stop=True)
                        return ins
                    P.op("pe", pvs, reads=vcb.bufs + PTb.bufs + Vnb.bufs + onesb.bufs, writes=[psb[bo], psb[bd]])
                    rec4 = A.view(T20, [128, 256], F32)
                    ot4 = A.view(T20 + KB, [128, 16, 16], F32)
                    P.op("dve", lambda e, bd=bd: e.reciprocal(out=rec4.ap, in_=psum[bd][:, 0:256]), reads=[psb[bd]], writes=rec4.bufs)
                    P.op("dve", lambda e, bo=bo: e.tensor_tensor(out=ot4.ap, in0=psum[bo][:, 0:256].rearrange("p (h q) -> p h q", h=16),
                                                                in1=rec4.ap.rearrange("p (h q) -> p h q", h=16), op=ALU.mult),
                         reads=[psb[bo]] + rec4.bufs, writes=ot4.bufs)
                    ps_free(bo)
                    ps_free(bd)
                    P.op("dve", lambda e, b=b: e.tensor_tensor(out=y_b.ap[:, :, 16 * b:16 * b + 16], in0=ot4.ap,
                                                              in1=szall.ap[:, :, 16 * b:16 * b + 16], op=ALU.mult),
                         reads=ot4.bufs + szall.bufs, writes=y_b.bufs)

            if DBG["stop"] == "B2":
                return
            if do_spill:
                P.dma("pool", lambda e: e.dma_start(out=kscr, in_=kT.ap.rearrange("p h w -> p (h w)")), s_spill[0],
                      reads=kT.bufs, writes=[kscr_b])
                P.dma("pool", lambda e: e.dma_start(out=vscr, in_=Vsb.ap.rearrange("p t f -> p (t f)")), s_spill[1],
                      reads=Vsb.bufs, writes=[vscr_b])

            for fb in range(32):
                sga = A.view(T10 + (fb % 2) * 8 * KB, [128, W], F32)
                sgb = A.view(T10 + (fb % 2) * 8 * KB + 2 * KB, [128, W], F32)
                c1 = A.view(T10 + (fb % 2) * 8 * KB + 4 * KB, [128, W], F32)
                c2 = A.view(T10 + (fb % 2) * 8 * KB + 6 * KB, [128, W], F32)
                b1 = proj_fm(hT, 32, "S")
                act_fn(sga, psv(b1)[:, 0:W], AF.Sigmoid)
                ps_free(b1)
                b2 = proj_fm(hT, 32, "S")
                act_fn(sgb, psv(b2)[:, 0:W], AF.Sigmoid)
                ps_free(b2)
                b3 = proj_fm(y_a, 16, "S2")
                tt(c1, psv(b3)[:, 0:W], sga, ALU.mult)
                ps_free(b3)
                b4 = proj_fm(y_b, 16, "S2")
                tt(c2, psv(b4)[:, 0:W], sgb, ALU.mult)
                ps_free(b4)
                tt(mT[:, fb, :], c1, c2, ALU.add)

            if DBG["stop"] == "C":
                return
            for fblk in range(8):
                def ev_r(bi, pv, fblk=fblk):
                    r0, nr = tbs[bi]
                    copy_op("dve", r_[0:nr, bi, fblk * 512:(fblk + 1) * 512], pv[0:nr])
                    act_fn(junk[0:nr], pv[0:nr], AF.Square, accum=V(ssq.ap[0:nr, bi, fblk:fblk + 1], ssq.bufs))
                proj_tm(mT, 4, ev_r)
            gpost_b = A.view(T10, [128, D], F32)
            bcast_load(gpost_b, gpost, s_gpost)
            for bi, (r0, nr) in enumerate(tbs):
                s1 = stat[0:nr, 8 + bi:9 + bi]
                P.op("dve", lambda e, s1=s1, bi=bi, nr=nr: e.reduce_sum(out=s1.ap, in_=ssq.ap[0:nr, bi, :], axis=mybir.AxisListType.X),
                     reads=ssq.bufs, writes=s1.bufs)
                r2 = stat[0:nr, 12 + bi:13 + bi]
                rstd_from(r2, s1, 1.0 / D, nr)
                for pc in range(4):
                    xpv = A.view(T20 + (pc % 2) * 4 * KB, [128, 1024], F32)
                    x1b = A.view(KV0 + (pc % 2) * 2 * KB, [128, 1024], BF16)
                    cs = slice(pc * 1024, (pc + 1) * 1024)
                    P.dma("sp", lambda e, o=xpv[0:nr].ap, s=xsrc[r0:r0 + nr, cs]: e.dma_start(out=o, in_=s), s_ld[pc % 2], writes=xpv.bufs)
                    rr = r_[0:nr, bi, cs]
                    P.op("dve", lambda e, rr=rr, r2=r2, nr=nr, cs=cs: e.scalar_tensor_tensor(out=rr.ap, in0=rr.ap, scalar=r2.ap,
                                                                                       in1=gpost_b.ap[0:nr, cs], op0=ALU.mult, op1=ALU.mult),
                         reads=rr.bufs + r2.bufs + gpost_b.bufs, writes=rr.bufs)
                    tt(rr, rr, xpv[0:nr], ALU.add)
                    copy_op("act", x1b[0:nr], rr)
                    pi = ps_alloc()
                    p16 = psv16(pi)

                    def trx1(e, p16=p16, nr=nr, x1b=x1b):
                        for j in range(8):
                            ins = e.transpose(out=p16.ap[:, j * nr:(j + 1) * nr], in_=x1b.ap[0:nr, j * 128:(j + 1) * 128],
                                              identity=identb.ap[0:nr, 0:nr])
                        return ins
                    P.op("pe", trx1, reads=x1b.bufs + identb.bufs, writes=p16.bufs)
                    copy_op(ev_eng(), x1T[:, pc * 8:(pc + 1) * 8, r0:r0 + nr],
                            V(p16.ap[:, 0:8 * nr].rearrange("p (k c) -> p k c", k=8), p16.bufs))
                    ps_free(pi)

            if DBG["stop"] == "D":
                return
            for fblk in range(8):
                ppslot = w_next("PP")
                ppv = ppslot.ap[:, 0:1024].rearrange("p (kc c) -> p kc c", c=512)
                pbanks = [ps_alloc() for _ in tbs]

                def ppm(e, pbanks=pbanks, ppv=ppv):
                    for bi, (r0, nr) in enumerate(tbs):
                        for kc in range(2):
                            ins = e.matmul(psum[pbanks[bi]][0:nr, :], lhsT=pT.ap[:, kc, r0:r0 + nr], rhs=ppv[:, kc, :], start=(kc == 0), stop=(kc == 1))
                    return ins
                P.op("pe", ppm, reads=pT.bufs + ppslot.bufs, writes=[psb[b] for b in pbanks])

                def ev_g(bi, pv, fblk=fblk, pbanks=pbanks):
                    r0, nr = tbs[bi]
                    sg = A.view(T20 + (bi % 2) * 4 * KB, [128, 512], F32)
                    tg = A.view(T20 + (bi % 2) * 4 * KB + 2 * KB, [128, 512], F32)
                    act_fn(sg[0:nr], pv[0:nr], AF.Sigmoid)
                    tt(tg[0:nr], psv(pbanks[bi])[0:nr], sg[0:nr], ALU.mult)
                    ps_free(pbanks[bi])
                    yy = r_[0:nr, bi, fblk * 512:(fblk + 1) * 512]
                    tt(yy, yy, tg[0:nr], ALU.add)
                proj_tm(x1T, 4, ev_g)
            for bi, (r0, nr) in enumerate(tbs):
                out_toks.append(P.dma("pool", lambda e, s=r_.ap[0:nr, bi, :], o=ysrc[r0:r0 + nr, :]: e.dma_start(out=o, in_=s),
                                      s_out, reads=r_.bufs))

        qall = A.view(KV0 + 2 * KB, [128, 16, 64], BF16)
        szall = A.view(KV0 + 4 * KB, [128, 16, 64], BF16)

        tl = DBG["tiles"] if DBG["tiles"] is not None else [("p", 0), ("p", 1), ("p", 2), ("p", 3), ("s", 0)]
        for (kd, ti) in tl:
            run_tile(kd, ti)

        for t in out_toks:
            P.wait_tok("sp", t)
        P.emit()
    return nc


_NC_CACHE = {}


def kernel(x_prompt, x_sample, cache_attn_k, cache_attn_v, p_prompt, p_sample,
           norm_pre_g, norm_post_g, w_in, gmlp_ln_g, gmlp_ln_b, gmlp_w_s, gmlp_b_s,
           attn_rel_bias, w_up_a, w_up_b, w_out, w_ple_gate, w_ple_proj):
    f = lambda a: np.ascontiguousarray(np.asarray(a, dtype=np.float32))
    ncores = 8
    if "nc" not in _NC_CACHE:
        _NC_CACHE["nc"] = build_program()
    nc = _NC_CACHE["nc"]
    shared = {
        "gpre": f(norm_pre_g[0]).reshape(1, D), "gpost": f(norm_post_g[0]).reshape(1, D),
        "w_in": f(w_in[0]), "lng": f(gmlp_ln_g[0]).reshape(1, DB), "lnb": f(gmlp_ln_b[0]).reshape(1, DB),
        "ws": f(gmlp_w_s[0]), "bs": f(gmlp_b_s[0]).reshape(1, 1024), "rb": f(attn_rel_bias[0]),
        "wua": f(w_up_a[0]), "wub": f(w_up_b[0]), "wo": f(w_out[0]), "wpg": f(w_ple_gate[0]), "wpp": f(w_ple_proj[0]),
        "ident": np.eye(128, dtype=np.float32),
    }
    in_maps = []
    for c in range(ncores):
        m = dict(shared)
        m["xp"] = f(x_prompt[c])
        m["xs"] = f(x_sample[4 * c:4 * c + 4]).reshape(64, D)
        m["ck"] = f(cache_attn_k[0, 4 * c:4 * c + 4]).reshape(4, 512, DB)
        m["cv"] = f(cache_attn_v[0, 4 * c:4 * c + 4]).reshape(4, 512, DB)
        m["ppr"] = f(p_prompt[0, c])
        m["psm"] = f(p_sample[0, 4 * c:4 * c + 4]).reshape(64, 256)
        in_maps.append(m)
    res = run_bass_kernel_spmd(nc, in_maps, core_ids=list(range(ncores)))
    R = res.results
    y_prompt = np.stack([R[c]["yp"] for c in range(ncores)]).reshape(8, 2048, D)
    y_sample = np.concatenate([R[c]["ys"].reshape(4, 16, D) for c in range(ncores)], axis=0)
    nkp = np.stack([R[c]["nkp"].reshape(512, NH, 128) for c in range(ncores)])[None]
    nvp = np.stack([R[c]["nvp"].reshape(512, NH, 128) for c in range(ncores)])[None]
    nks = np.concatenate([R[c]["nks"].reshape(4, 16, NH, 128) for c in range(ncores)], axis=0)[None]
    nvs = np.concatenate([R[c]["nvs"].reshape(4, 16, NH, 128) for c in range(ncores)], axis=0)[None]
    gvp = np.stack([R[c]["gvp"] for c in range(ncores)])[None]
    gvs = np.concatenate([R[c]["gvs"].reshape(4, 16, DB) for c in range(ncores)], axis=0)[None]
    return (y_prompt.astype(np.float32), y_sample.astype(np.float32), nkp.astype(np.float32), nvp.astype(np.float32),
            nks.astype(np.float32), nvs.astype(np.float32), gvp.astype(np.float32), gvs.astype(np.float32))
```
